# Optimizing a Trainium2 kernel written in Bass

```python
import math
import jax, jax.numpy as jnp
from jax import lax
import numpy as np

D_MODEL = 1024
BATCH = 4
SEQ = 4096
DEPTH = 2

N_MIXERS = 2
EPS = 1e-6
ATT_Q_HEADS = 16
ATT_KV_HEADS = 4
ATT_GROUP = ATT_Q_HEADS // ATT_KV_HEADS
ATT_HEAD_DIM = 64
WINDOW = 128
ATT_BLOCK = 128
REL_BUCKETS = 32
REL_MAX_DIST = 128
RET_HEADS = 4
RET_QK_DIM = D_MODEL // RET_HEADS
RET_V_DIM = 2 * RET_QK_DIM
RET_CHUNK = 128
ROPE_BASE = 10000.0
FFN_HIDDEN = -(-8 * D_MODEL // (3 * 256)) * 256

kernel_name = 'hybrid_swa_sink_retention_encoder'


def rmsnorm(x, g):
    xf = x.astype(jnp.float32)
    y = xf * lax.rsqrt(jnp.mean(xf * xf, axis=-1, keepdims=True) + EPS)
    return (y * g.astype(jnp.float32)).astype(x.dtype)


def t5_bucket(rel):
    nb = REL_BUCKETS // 2
    ret = jnp.where(rel > 0, nb, 0)
    n = jnp.abs(rel)
    max_exact = nb // 2
    nf = jnp.maximum(n, 1).astype(jnp.float32)
    large = max_exact + (jnp.log(nf / max_exact) / math.log(REL_MAX_DIST / max_exact)
                         * (nb - max_exact)).astype(jnp.int32)
    large = jnp.minimum(large, nb - 1)
    return ret + jnp.where(n < max_exact, n, large)


def windowed_gqa_sink(h, w_qkv, w_o, sink, rel_bias):
    B, S, _ = h.shape
    L = ATT_BLOCK
    nb = S // L
    dh = ATT_HEAD_DIM
    qkv = h @ w_qkv
    q, k, v = jnp.split(qkv, [ATT_Q_HEADS * dh, (ATT_Q_HEADS + ATT_KV_HEADS) * dh], axis=-1)
    q = q.reshape(B, nb, L, ATT_KV_HEADS, ATT_GROUP, dh)
    pad = ((0, 0), (L, L), (0, 0), (0, 0))
    kp = jnp.pad(k.reshape(B, S, ATT_KV_HEADS, dh), pad).reshape(B, nb + 2, L, ATT_KV_HEADS, dh)
    vp = jnp.pad(v.reshape(B, S, ATT_KV_HEADS, dh), pad).reshape(B, nb + 2, L, ATT_KV_HEADS, dh)
    k_band = jnp.concatenate([kp[:, :-2], kp[:, 1:-1], kp[:, 2:]], axis=2)
    v_band = jnp.concatenate([vp[:, :-2], vp[:, 1:-1], vp[:, 2:]], axis=2)
    scores = jnp.einsum('bnqhgd,bnkhd->bnhgqk', q, k_band).astype(jnp.float32) * (dh ** -0.5)
    t = jnp.arange(L, dtype=jnp.int32)
    j = jnp.arange(3 * L, dtype=jnp.int32)
    rel = j[None, :] - L - t[:, None]
    bias = rel_bias.astype(jnp.float32)[t5_bucket(rel)]
    bias = jnp.transpose(bias, (2, 0, 1)).reshape(ATT_KV_HEADS, ATT_GROUP, L, 3 * L)
    in_win = jnp.abs(rel) <= WINDOW
    k_abs = jnp.arange(nb, dtype=jnp.int32)[:, None] * L - L + j[None, :]
    in_range = (k_abs >= 0) & (k_abs < S)
    mask = in_win[None] & in_range[:, None, :]
    logits = jnp.where(mask[None, :, None, None], scores + bias[None, None], jnp.float32(-1e30))
    sink_l = jnp.broadcast_to(sink.astype(jnp.float32).reshape(1, 1, ATT_KV_HEADS, ATT_GROUP, 1, 1),
                              logits.shape[:-1] + (1,))
    p = jax.nn.softmax(jnp.concatenate([logits, sink_l], axis=-1), axis=-1)[..., :-1]
    out = jnp.einsum('bnhgqk,bnkhd->bnqhgd', p.astype(v_band.dtype), v_band)
    return out.reshape(B, S, ATT_Q_HEADS * dh) @ w_o


def rotary(x):
    S, d = x.shape[1], x.shape[-1]
    inv = ROPE_BASE ** (-jnp.arange(0, d, 2, dtype=jnp.float32) / d)
    ang = jnp.arange(S, dtype=jnp.float32)[:, None] * inv[None]
    cos = jnp.cos(ang)[None, :, None, :]
    sin = jnp.sin(ang)[None, :, None, :]
    xf = x.astype(jnp.float32)
    x1, x2 = xf[..., : d // 2], xf[..., d // 2:]
    return jnp.concatenate([x1 * cos - x2 * sin, x1 * sin + x2 * cos], axis=-1).astype(x.dtype)


def retention_chunkwise(q, k, v, log_gamma, include_diag):
    B, S, H, dk = q.shape
    dv = v.shape[-1]
    L = RET_CHUNK
    C = S // L
    idx = jnp.arange(L, dtype=jnp.float32)
    diff = idx[:, None] - idx[None, :]
    keep = (diff >= 0) if include_diag else (diff > 0)
    decay_intra = jnp.where(keep[None], jnp.exp(log_gamma[:, None, None] * jnp.maximum(diff, 0.0)[None]), 0.0)
    xi = jnp.exp(log_gamma[:, None] * (idx + 1.0)[None])[None, :, :, None]
    zeta = jnp.exp(log_gamma[:, None] * (L - 1.0 - idx)[None])[None, :, :, None]
    chunk_decay = jnp.exp(log_gamma * L)[None, :, None, None]

    def to_chunks(a):
        return a.reshape(B, C, L, H, a.shape[-1]).transpose(1, 0, 3, 2, 4).astype(jnp.float32)

    def step(state, inp):
        qi, ki, vi = inp
        s = jnp.einsum('bhld,bhmd->bhlm', qi, ki) * decay_intra[None]
        inner = jnp.einsum('bhlm,bhme->bhle', s, vi)
        cross = jnp.einsum('bhld,bhde->bhle', qi * xi, state)
        new_state = state * chunk_decay + jnp.einsum('bhmd,bhme->bhde', ki * zeta, vi)
        return new_state, inner + cross

    state0 = jnp.zeros((B, H, dk, dv), jnp.float32)
    _, out = lax.scan(step, state0, (to_chunks(q), to_chunks(k), to_chunks(v)))
    return out.transpose(1, 0, 3, 2, 4).reshape(B, S, H, dv)


def bidir_retention(h, w_in, w_o, decay_logit_fwd, decay_logit_bwd):
    B, S, _ = h.shape
    H, dk, dv = RET_HEADS, RET_QK_DIM, RET_V_DIM
    proj = h @ w_in
    q, k, v, g = jnp.split(proj, [H * dk, 2 * H * dk, 2 * H * dk + H * dv], axis=-1)
    q = rotary(q.reshape(B, S, H, dk))
    k = rotary(k.reshape(B, S, H, dk) * (dk ** -0.5))
    v = v.reshape(B, S, H, dv)
    lg_f = jax.nn.log_sigmoid(decay_logit_fwd.astype(jnp.float32))
    lg_b = jax.nn.log_sigmoid(decay_logit_bwd.astype(jnp.float32))
    y_f = retention_chunkwise(q, k, v, lg_f, True)
    y_b = retention_chunkwise(q[:, ::-1], k[:, ::-1], v[:, ::-1], lg_b, False)[:, ::-1]
    y = y_f + y_b
    y = y * lax.rsqrt(jnp.mean(y * y, axis=-1, keepdims=True) + EPS)
    y = y.reshape(B, S, H * dv).astype(h.dtype)
    return (jax.nn.silu(g) * y) @ w_o


def swiglu(h, w_in, w_out):
    a, b = jnp.split(h @ w_in, 2, axis=-1)
    return (jax.nn.silu(a) * b) @ w_out


def setup_inputs(seed: int = 0) -> dict:
    key = jax.random.key(seed)
    ks = jax.random.split(key, 20)
    D = D_MODEL
    n_att = (DEPTH + N_MIXERS - 1) // N_MIXERS
    n_ret = DEPTH // N_MIXERS
    f32 = jnp.float32

    def w(k, shape, fan_in, scale=1.0):
        return jax.random.normal(k, shape, f32) * (scale * fan_in ** -0.5)

    att_qkv_w = (ATT_Q_HEADS + 2 * ATT_KV_HEADS) * ATT_HEAD_DIM
    ret_in_w = 2 * RET_HEADS * RET_QK_DIM + 2 * RET_HEADS * RET_V_DIM
    gam = 1.0 - np.exp(np.linspace(math.log(1 / 32), math.log(1 / 512), RET_HEADS))
    base_logit = jnp.asarray(np.log(gam / (1.0 - gam)), f32)
    return {
        'x': jax.random.normal(ks[0], (BATCH, SEQ, D), f32),
        'c': jax.random.normal(ks[1], (BATCH, D), f32),
        'rel_bias': jax.random.normal(ks[2], (REL_BUCKETS, ATT_Q_HEADS), f32) * 0.5,
        'att_w_qkv': w(ks[3], (n_att, D, att_qkv_w), D),
        'att_w_o': w(ks[4], (n_att, ATT_Q_HEADS * ATT_HEAD_DIM, D), ATT_Q_HEADS * ATT_HEAD_DIM),
        'att_sink': jax.random.normal(ks[5], (n_att, ATT_Q_HEADS), f32),
        'ret_w_in': w(ks[6], (n_ret, D, ret_in_w), D),
        'ret_w_o': w(ks[7], (n_ret, RET_HEADS * RET_V_DIM, D), RET_HEADS * RET_V_DIM),
        'ret_decay_fwd': base_logit[None] + 0.1 * jax.random.normal(ks[8], (n_ret, RET_HEADS), f32),
        'ret_decay_bwd': base_logit[None] + 0.1 * jax.random.normal(ks[9], (n_ret, RET_HEADS), f32),
        'ada_w': w(ks[10], (DEPTH, D, 6 * D), D, 0.5),
        'ada_b': 0.02 * jax.random.normal(ks[11], (DEPTH, 6 * D), f32),
        'mix_norm_pre': 1.0 + 0.05 * jax.random.normal(ks[12], (DEPTH, D), f32),
        'mix_norm_post': 1.0 + 0.05 * jax.random.normal(ks[13], (DEPTH, D), f32),
        'ffn_norm_pre': 1.0 + 0.05 * jax.random.normal(ks[14], (DEPTH, D), f32),
        'ffn_norm_post': 1.0 + 0.05 * jax.random.normal(ks[15], (DEPTH, D), f32),
        'ffn_w_in': w(ks[16], (DEPTH, D, 2 * FFN_HIDDEN), D),
        'ffn_w_out': w(ks[17], (DEPTH, FFN_HIDDEN, D), FFN_HIDDEN),
    }


def reference(x, c, rel_bias, att_w_qkv, att_w_o, att_sink, ret_w_in, ret_w_o,
              ret_decay_fwd, ret_decay_bwd, ada_w, ada_b, mix_norm_pre, mix_norm_post,
              ffn_norm_pre, ffn_norm_post, ffn_w_in, ffn_w_out):
    c_act = jax.nn.silu(c)
    for i in range(DEPTH):
        mod = (c_act @ ada_w[i] + ada_b[i])[:, None, :]
        sh1, sc1, g1, sh2, sc2, g2 = jnp.split(mod, 6, axis=-1)
        h = rmsnorm(x, mix_norm_pre[i]) * (1.0 + sc1) + sh1
        j = i // N_MIXERS
        if i % N_MIXERS == 0:
            y = windowed_gqa_sink(h, att_w_qkv[j], att_w_o[j], att_sink[j], rel_bias)
        else:
            y = bidir_retention(h, ret_w_in[j], ret_w_o[j], ret_decay_fwd[j], ret_decay_bwd[j])
        x = x + g1 * rmsnorm(y, mix_norm_post[i])
        h = rmsnorm(x, ffn_norm_pre[i]) * (1.0 + sc2) + sh2
        y = swiglu(h, ffn_w_in[i], ffn_w_out[i])
        x = x + g2 * rmsnorm(y, ffn_norm_post[i])
    return x
```

```python
import math
from contextlib import ExitStack

import numpy as np
import concourse.bass as bass
import concourse.mybir as mybir
from concourse.bass_utils import run_bass_kernel_spmd

F32 = mybir.dt.float32
BF16 = mybir.dt.bfloat16
AF = mybir.ActivationFunctionType
ALU = mybir.AluOpType

NCORES = 8
TOK = 2048
NT = 4
D = 1024
KC = 8
FH = 2816
NJ = 22
EPS = 1e-6
ENGS = ("pe", "act", "dve", "pool", "sp")
RET_STOP = ""
DEBUG_DUMP = False


class Buf:
    def __init__(self, name):
        self.name = name
        self.w = {}
        self.r = {}


class Prog:
    def __init__(self, nc, es):
        self.nc = nc
        self.es = es
        self.q = {e: [] for e in ENGS}
        self.semh = {}
        self.semc = {}
        for e in ("pe", "act", "dve", "pool"):
            self.newsem(e)
        self.dma_pool = {}
        self.dma_idx = {}
        for qn, n in (("sp", 20), ("pool", 12), ("act", 4)):
            names = []
            for i in range(n):
                nm = "d_%s_%d" % (qn, i)
                self.newsem(nm)
                names.append(nm)
            self.dma_pool[qn] = names
            self.dma_idx[qn] = 0

    def newsem(self, name):
        self.semh[name] = self.es.enter_context(self.nc.semaphore(name))
        self.semc[name] = 0

    @staticmethod
    def _deps(reads, writes):
        d = {}
        for b in reads:
            for k, v in b.w.items():
                d[k] = max(d.get(k, 0), v)
        for b in writes:
            for k, v in b.w.items():
                d[k] = max(d.get(k, 0), v)
            for k, v in b.r.items():
                d[k] = max(d.get(k, 0), v)
        return d

    @staticmethod
    def _mark(ev, reads, writes):
        for b in writes:
            b.w = {ev[0]: ev[1]}
            b.r = {}
        for b in reads:
            if b not in writes:
                b.r[ev[0]] = max(b.r.get(ev[0], 0), ev[1])

    def run(self, eng, fns, reads=(), writes=(), extra=()):
        if not isinstance(fns, (list, tuple)):
            fns = [fns]
        d = self._deps(reads, writes)
        for k, v in extra:
            d[k] = max(d.get(k, 0), v)
        if eng == "pe":
            d.pop(eng, None)
        self.semc[eng] += 1
        ev = (eng, self.semc[eng])
        self.q[eng].append((d, list(fns), eng, 1))
        self._mark(ev, reads, writes)
        return ev

    def dma(self, queue, fn, reads=(), writes=(), extra=()):
        pool = self.dma_pool[queue]
        name = pool[self.dma_idx[queue] % len(pool)]
        self.dma_idx[queue] += 1
        d = self._deps(reads, writes)
        for k, v in extra:
            d[k] = max(d.get(k, 0), v)
        if self.semc[name] > 0:
            d[name] = max(d.get(name, 0), self.semc[name])
        self.semc[name] += 16
        ev = (name, self.semc[name])
        self.q[queue].append((d, [fn], name, 16))
        self._mark(ev, reads, writes)
        return ev

    def wait_all(self, eng):
        d = {k: v for k, v in self.semc.items() if v > 0 and k != eng}
        self.q[eng].append((d, [], None, 0))

    def barrier(self):
        snap = {k: v for k, v in self.semc.items() if v > 0}
        for e in ENGS:
            d = dict(snap)
            d.pop(e, None)
            self.q[e].append((d, [], None, 0))

    def flush(self):
        nc = self.nc
        eobj = {"pe": nc.tensor, "act": nc.scalar, "dve": nc.vector, "pool": nc.gpsimd, "sp": nc.sync}

        def emit(en):
            e = eobj[en]
            waited = {}
            for d, fns, sname, inc in self.q[en]:
                for k, v in d.items():
                    if waited.get(k, 0) < v:
                        e.wait_ge(self.semh[k], v)
                        waited[k] = v
                ins = None
                for f in fns:
                    ins = f(e)
                if sname is not None:
                    ins.then_inc(self.semh[sname], inc)

        with nc.Block() as block:
            @block.tensor
            def _(e):
                emit("pe")

            @block.scalar
            def _(e):
                emit("act")

            @block.vector
            def _(e):
                emit("dve")

            @block.gpsimd
            def _(e):
                emit("pool")

            @block.sync
            def _(e):
                emit("sp")


def build_program(stages, mode="full"):
    nc = bass.Bass("TRN2", target_bir_lowering=False)
    es = ExitStack()
    P = Prog(nc, es)

    in_names = []

    def dram(name, shape, dt=F32, kind="ExternalInput"):
        if kind == "Scratch":
            kind = {"full": "Internal", "L1": "ExternalOutput", "L2": "ExternalInput"}[mode]
        if kind == "ExternalInput":
            in_names.append(name)
        return nc.dram_tensor(name, list(shape), dt, kind=kind).ap()

    def sb(name, shape, dt=F32):
        return es.enter_context(nc.sbuf_tensor(name, list(shape), dt))

    x_d = dram("x", [TOK + 128, D])
    out_d = dram("out", [TOK, D], kind="ExternalOutput")
    ident_d = dram("ident", [128, 128])
    cT_d = dram("cT", [128, KC])
    ada_w_d = dram("ada_w", [2, D, 6 * D])
    ada_b_d = dram("ada_bT", [2, 128, 48])
    norms_d = dram("normsT", [2, 4, 128, KC])
    ffn_w_in_d = dram("ffn_w_in", [2, D, 2 * FH])
    ffn_w_out_d = dram("ffn_w_out", [2, FH, D])

    xT = sb("xT", [128, KC, TOK + 128], F32)
    xT_b = [Buf("xT%d" % t) for t in range(NT + 1)]
    ident_f = sb("ident_f", [128, 128], F32)
    ones_bf = sb("ones_bf", [128, 128], BF16)
    cact = sb("cact", [128, KC, 2], F32)
    modv = sb("modv", [128, 2, 48], F32)
    normsT = sb("normsT_s", [128, 2, 4, KC], F32)
    coef = sb("coef", [128, 2, 4, KC], F32)
    const_b = Buf("const")
    coef_b = Buf("coef")

    ps = [es.enter_context(nc.psum_tensor("ps%d" % i, [128, 512], F32)) for i in range(8)]
    ps_b = [Buf("ps%d" % i) for i in range(8)]

    def phase_setup():
        with ExitStack() as ph:
            def psb(name, shape, dt=F32):
                return ph.enter_context(nc.sbuf_tensor(name, list(shape), dt))

            xin = [psb("xin%d" % i, [128, D], F32) for i in range(3)]
            xin_b = [Buf("xin%d" % i) for i in range(3)]
            adw = [psb("adw%d" % i, [128, KC, 1024], F32) for i in range(2)]
            adw_b = [Buf("adw%d" % i) for i in range(2)]
            cTs = psb("cTs", [128, KC], F32)
            adb = psb("adb", [128, 2, 48], F32)

            P.dma("sp", lambda e: e.dma_start(out=ident_f[:], in_=ident_d), writes=[const_b])
            P.dma("sp", lambda e: e.dma_start(out=cTs[:], in_=cT_d), writes=[const_b])
            P.dma("sp", lambda e: e.dma_start(out=adb[:], in_=ada_b_d.rearrange("l p j -> p l j")), writes=[const_b])
            P.dma("sp", lambda e: e.dma_start(out=normsT[:], in_=norms_d.rearrange("l k p c -> p l k c")),
                  writes=[const_b])
            P.run("dve", lambda e: e.memset(ones_bf[:], 1.0), writes=[const_b])
            P.run("act", [lambda e: e.activation(out=cact[:, :, 0], in_=cTs[:], func=AF.Silu),
                          lambda e: e.activation(out=cact[:, :, 1], in_=cTs[:], func=AF.Silu)],
                  reads=[const_b], writes=[coef_b])

            for tb in range(17):
                s = tb % 3
                P.dma("sp", lambda e, s=s, tb=tb: e.dma_start(out=xin[s][:], in_=x_d[tb * 128:(tb + 1) * 128, :]),
                      writes=[xin_b[s]])
                for half in range(2):
                    bk = (tb * 2 + half) % 4
                    fns = []
                    for i in range(4):
                        c = half * 4 + i
                        fns.append(lambda e, bk=bk, i=i, c=c, s=s: e.transpose(
                            ps[bk][:, i * 128:(i + 1) * 128], xin[s][:, c * 128:(c + 1) * 128], ident_f[:]))
                    P.run("pe", fns, reads=[xin_b[s], const_b], writes=[ps_b[bk]])
                    eng = "dve" if half == 0 else "act"
                    dst = xT[:, half * 4:(half + 1) * 4, tb * 128:(tb + 1) * 128]
                    src = ps[bk][:, :].rearrange("p (c t) -> p c t", c=4)
                    if eng == "dve":
                        P.run("dve", lambda e, dst=dst, src=src: e.tensor_copy(out=dst, in_=src),
                              reads=[ps_b[bk]], writes=[xT_b[tb // 4]])
                    else:
                        P.run("act", lambda e, dst=dst, src=src: e.activation(out=dst, in_=src, func=AF.Copy),
                              reads=[ps_b[bk]], writes=[xT_b[tb // 4]])

            for l in range(2):
                bk = 4 + l
                for g in range(6):
                    s = (l * 6 + g) % 2
                    P.dma("sp", lambda e, s=s, l=l, g=g: e.dma_start(
                        out=adw[s][:], in_=ada_w_d[l, :, g * 1024:(g + 1) * 1024].rearrange("(kc p) n -> p kc n", p=128)),
                        writes=[adw_b[s]])
                    fns = []
                    for j in range(8):
                        n = g * 8 + j
                        for kc in range(KC):
                            fns.append(lambda e, s=s, j=j, kc=kc, n=n, bk=bk: e.matmul(
                                ps[bk][:, 2 * n:2 * n + 2], lhsT=adw[s][:, kc, j * 128:(j + 1) * 128],
                                rhs=cact[:, kc, :], start=(kc == 0), stop=(kc == KC - 1)))
                    P.run("pe", fns, reads=[adw_b[s], coef_b], writes=[ps_b[bk]])
                src = ps[bk][:, 0:96].rearrange("p (n two) -> p n two", two=2)[:, :, 0]
                P.run("dve", lambda e, l=l, src=src: e.tensor_tensor(out=modv[:, l, :], in0=src, in1=adb[:, l, :],
                                                                    op=ALU.add),
                      reads=[ps_b[bk], const_b], writes=[coef_b])
                fns = [
                    lambda e, l=l: e.scalar_tensor_tensor(out=coef[:, l, 0, :], in0=modv[:, l, 8:16], scalar=1.0,
                                                          in1=normsT[:, l, 0, :], op0=ALU.add, op1=ALU.mult),
                    lambda e, l=l: e.tensor_tensor(out=coef[:, l, 1, :], in0=modv[:, l, 16:24], in1=normsT[:, l, 1, :],
                                                   op=ALU.mult),
                    lambda e, l=l: e.scalar_tensor_tensor(out=coef[:, l, 2, :], in0=modv[:, l, 32:40], scalar=1.0,
                                                          in1=normsT[:, l, 2, :], op0=ALU.add, op1=ALU.mult),
                    lambda e, l=l: e.tensor_tensor(out=coef[:, l, 3, :], in0=modv[:, l, 40:48], in1=normsT[:, l, 3, :],
                                                   op=ALU.mult),
                ]
                P.run("dve", fns, reads=[const_b], writes=[coef_b])
            P.barrier()

    def sumsq_to_rstd(sq, sq_b, rstd, rstd_b, bank, tw=512):
        fns = [lambda e, kc=kc: e.matmul(ps[bank][:, 0:tw], lhsT=ones_bf[:], rhs=sq[:, kc, 0:tw],
                                         start=(kc == 0), stop=(kc == KC - 1)) for kc in range(KC)]
        P.run("pe", fns, reads=[sq_b, const_b], writes=[ps_b[bank]])
        P.run("act", [lambda e: e.activation(out=rstd[:, 0:tw], in_=ps[bank][:, 0:tw], func=AF.Ln, scale=1.0 / D, bias=eps_t[:]),
                      lambda e: e.activation(out=rstd[:, 0:tw], in_=rstd[:, 0:tw], func=AF.Exp, scale=-0.5)],
              reads=[ps_b[bank], const_b], writes=[rstd_b])

    def prenorm(t, l, which, hT_dst, hT_b, W, tw=512):
        t0 = t * 512
        xs = xT[:, :, t0:t0 + tw]
        gi = 0 if which == 0 else 2
        so = 0 if which == 0 else 24
        P.run("act", lambda e: e.activation(out=W["sq"][:, :, 0:tw], in_=xs, func=AF.Square),
              reads=[xT_b[t]], writes=[W["sq_b"]])
        sumsq_to_rstd(W["sq"], W["sq_b"], W["rstd"], W["rstd_b"], W["nbank"], tw)
        fns = [lambda e, kc=kc: e.scalar_tensor_tensor(out=W["tmp"][:, kc, 0:tw], in0=xT[:, kc, t0:t0 + tw],
                                                       scalar=coef[:, l, gi, kc:kc + 1], in1=W["rstd"][:, 0:tw],
                                                       op0=ALU.mult, op1=ALU.mult) for kc in range(KC)]
        P.run("dve", fns, reads=[xT_b[t], W["rstd_b"], coef_b], writes=[W["tmp_b"]])
        fns = [lambda e, kc=kc: e.activation(out=hT_dst[:, kc, 0:tw], in_=W["tmp"][:, kc, 0:tw], func=AF.Identity,
                                             bias=modv[:, l, so + kc:so + kc + 1]) for kc in range(KC)]
        P.run("act", fns, reads=[W["tmp_b"], coef_b], writes=[hT_b])

    def postnorm_residual(t, l, which, W):
        gi = 1 if which == 0 else 3
        sumsq_to_rstd(W["sq"], W["sq_b"], W["rstd"], W["rstd_b"], W["nbank"])
        fns = [lambda e, kc=kc: e.scalar_tensor_tensor(out=W["tmp"][:, kc, :], in0=W["tmp"][:, kc, :],
                                                       scalar=coef[:, l, gi, kc:kc + 1], in1=W["rstd"][:],
                                                       op0=ALU.mult, op1=ALU.mult) for kc in range(KC)]
        P.run("dve", fns, reads=[W["rstd_b"], coef_b], writes=[W["tmp_b"]])
        xs = xT[:, :, t * 512:(t + 1) * 512]
        P.run("pool", lambda e: e.tensor_tensor(out=xs, in0=xs, in1=W["tmp"][:], op=ALU.add),
              reads=[W["tmp_b"]], writes=[xT_b[t]])

    eps_t = sb("eps_t", [128, 1], F32)
    P.run("dve", lambda e: e.memset(eps_t[:], EPS), writes=[const_b])

    def phase_ffn(l):
        with ExitStack() as ph:
            def psb(name, shape, dt=F32):
                return ph.enter_context(nc.sbuf_tensor("l%d_%s" % (l, name), list(shape), dt))

            hT = psb("f_hT", [128, KC, 1024], BF16)
            hT_b = [Buf("f_hT%d" % i) for i in range(2)]
            actT = psb("f_actT", [128, NJ, 1024], BF16)
            act_b = [[Buf("f_act%d_%d" % (j, s)) for s in range(2)] for j in range(NJ)]
            NWI = 2
            wi = [psb("f_wi%d" % i, [128, KC, 512], BF16) for i in range(NWI)]
            wi_b = [Buf("f_wi%d" % i) for i in range(NWI)]
            NWO = 3
            wo = [psb("f_wo%d" % i, [128, NJ, 128], BF16) for i in range(NWO)]
            wo_b = [Buf("f_wo%d" % i) for i in range(NWO)]
            W = dict(sq=psb("f_sq", [128, KC, 512], BF16), sq_b=Buf("f_sq"),
                     tmp=psb("f_tmp", [128, KC, 512], F32), tmp_b=Buf("f_tmp"),
                     rstd=psb("f_rstd", [128, 512], F32), rstd_b=Buf("f_rstd"), nbank=7)
            sa = [psb("f_sa%d" % i, [128, 512], F32) for i in range(2)]
            sa_b = [Buf("f_sa%d" % i) for i in range(2)]
            w_in_l = ffn_w_in_d[l]
            w_out_l = ffn_w_out_d[l]
            wo_cnt = 0
            wi_cnt = 0
            for pp in range(2):
                for s in range(2):
                    prenorm(2 * pp + s, l, 1, hT[:, :, s * 512:(s + 1) * 512], hT_b[s], W)
                it = 0
                for p in range(NJ // 2):
                    slot = wi_cnt % NWI
                    wi_cnt += 1
                    for ab in range(2):
                        c0 = ab * FH + p * 256
                        P.dma("pool", lambda e, slot=slot, ab=ab, c0=c0: e.dma_start(
                            out=wi[slot][:, :, ab * 256:(ab + 1) * 256],
                            in_=w_in_l[:, c0:c0 + 256].rearrange("(kc p) n -> p kc n", p=128)),
                            writes=[wi_b[slot]])
                    for jj in range(2):
                        j = 2 * p + jj
                        for s in range(2):
                            bA = (it % 2) * 2
                            bB = bA + 1
                            si = it % 2
                            it += 1
                            for ab, bk in ((0, bA), (1, bB)):
                                fns = [lambda e, kc=kc, slot=slot, ab=ab, jj=jj, s=s, bk=bk: e.matmul(
                                    ps[bk][:, :], lhsT=wi[slot][:, kc, ab * 256 + jj * 128: ab * 256 + (jj + 1) * 128],
                                    rhs=hT[:, kc, s * 512:(s + 1) * 512], start=(kc == 0), stop=(kc == KC - 1))
                                    for kc in range(KC)]
                                P.run("pe", fns, reads=[wi_b[slot], hT_b[s]], writes=[ps_b[bk]])
                            P.run("act", lambda e, bA=bA, si=si: e.activation(out=sa[si][:], in_=ps[bA][:, :], func=AF.Silu),
                                  reads=[ps_b[bA]], writes=[sa_b[si]])
                            P.run("dve", lambda e, bB=bB, si=si, j=j, s=s: e.tensor_tensor(
                                out=actT[:, j, s * 512:(s + 1) * 512], in0=ps[bB][:, :], in1=sa[si][:], op=ALU.mult),
                                reads=[ps_b[bB], sa_b[si]], writes=[act_b[j][s]])
                for s in range(2):
                    t = 2 * pp + s
                    for oc in range(KC):
                        slot = wo_cnt % NWO
                        wo_cnt += 1
                        P.dma("pool", lambda e, slot=slot, oc=oc: e.dma_start(
                            out=wo[slot][:], in_=w_out_l[:, oc * 128:(oc + 1) * 128].rearrange("(j p) n -> p j n", p=128)),
                            writes=[wo_b[slot]])
                        bk = 4 + (oc % 2)
                        fns = [lambda e, j=j, slot=slot, s=s, bk=bk: e.matmul(
                            ps[bk][:, :], lhsT=wo[slot][:, j, :], rhs=actT[:, j, s * 512:(s + 1) * 512],
                            start=(j == 0), stop=(j == NJ - 1)) for j in range(NJ)]
                        P.run("pe", fns, reads=[wo_b[slot]] + [act_b[j][s] for j in range(NJ)], writes=[ps_b[bk]])
                        P.run("act", [lambda e, oc=oc, bk=bk: e.activation(out=W["tmp"][:, oc, :], in_=ps[bk][:, :], func=AF.Copy),
                                      lambda e, oc=oc, bk=bk: e.activation(out=W["sq"][:, oc, :], in_=ps[bk][:, :], func=AF.Square)],
                              reads=[ps_b[bk]], writes=[W["tmp_b"], W["sq_b"]])
                    postnorm_residual(t, l, 1, W)
            P.barrier()


    att_wqkv_d = dram("att_w_qkv", [D, 1536])
    att_wo_d = dram("att_w_o", [D, D])
    sinkT_d = dram("sinkT", [128, KC])
    biasT_d = dram("biasT", [128, 3, 2048])

    def phase_att():
        with ExitStack() as ph:
            def psb(name, shape, dt=F32):
                return ph.enter_context(nc.sbuf_tensor(name, list(shape), dt))

            qT_all = psb("a_qT", [128, KC, TOK], BF16)
            q_b = [Buf("a_q%d" % i) for i in range(4)]
            kT_all = psb("a_kT", [128, 4, TOK + 128], BF16)
            k_b = [Buf("a_k%d" % i) for i in range(5)]
            V_all = psb("a_V", [128, 17, 256], BF16)
            v_b = [Buf("a_v%d" % i) for i in range(5)]
            esink = psb("a_esink", [128, KC], F32)
            esink_b = Buf("a_esink")
            W = dict(sq=psb("a_sq", [128, KC, 512], BF16), sq_b=Buf("a_sq"),
                     tmp=psb("a_tmp", [128, KC, 512], F32), tmp_b=Buf("a_tmp"),
                     rstd=psb("a_rstd", [128, 512], F32), rstd_b=Buf("a_rstd"), nbank=7)
            P.dma("sp", lambda e: e.dma_start(out=esink[:], in_=sinkT_d), writes=[esink_b])
            P.run("act", lambda e: e.activation(out=esink[:], in_=esink[:], func=AF.Exp), writes=[esink_b])
            with ExitStack() as ph1:
                def psb1(name, shape, dt=F32):
                    return ph1.enter_context(nc.sbuf_tensor(name, list(shape), dt))
                wq = psb1("a_wq", [128, KC, 1024], BF16)
                wk = psb1("a_wk", [128, KC, 4, 128], BF16)
                wv = psb1("a_wv", [128, KC, 256], BF16)
                hT = psb1("a_hT", [128, KC, 512], BF16)
                hT_b = Buf("a_hT")
                w_b = Buf("a_w")
                for half in range(2):
                    P.dma("pool", lambda e, half=half: e.dma_start(
                        out=wq[:, :, half * 512:(half + 1) * 512],
                        in_=att_wqkv_d[:, half * 512:(half + 1) * 512].rearrange("(kc p) n -> p kc n", p=128)),
                        writes=[w_b])
                for half in range(2):
                    for hk in range(4):
                        P.dma("pool", lambda e, half=half, hk=hk: e.dma_start(
                            out=wk[:, :, hk, half * 64:(half + 1) * 64],
                            in_=att_wqkv_d[:, 1024 + hk * 64:1024 + (hk + 1) * 64].rearrange("(kc p) n -> p kc n", p=128)),
                            writes=[w_b])
                P.dma("pool", lambda e: e.dma_start(
                    out=wv[:], in_=att_wqkv_d[:, 1280:1536].rearrange("(kc p) n -> p kc n", p=128)), writes=[w_b])
                ev_i = 0
                for t in range(5):
                    tw = 512 if t < 4 else 128
                    prenorm(t, 0, 0, hT, hT_b, W, tw)
                    if t < 4:
                        for c in range(KC):
                            bk = c % 2
                            fns = [lambda e, kc=kc, c=c, bk=bk: e.matmul(
                                ps[bk][:, :], lhsT=wq[:, kc, c * 128:(c + 1) * 128], rhs=hT[:, kc, :],
                                start=(kc == 0), stop=(kc == KC - 1)) for kc in range(KC)]
                            P.run("pe", fns, reads=[w_b, hT_b], writes=[ps_b[bk]])
                            dst = qT_all[:, c, t * 512:(t + 1) * 512]
                            if c % 2 == 0:
                                P.run("act", lambda e, dst=dst, bk=bk: e.activation(out=dst, in_=ps[bk][:, :], func=AF.Copy, scale=0.125),
                                      reads=[ps_b[bk]], writes=[q_b[t]])
                            else:
                                P.run("dve", lambda e, dst=dst, bk=bk: e.tensor_scalar(out=dst, in0=ps[bk][:, :], scalar1=0.125, scalar2=None, op0=ALU.mult),
                                      reads=[ps_b[bk]], writes=[q_b[t]])
                    for hk in range(4):
                        bk = 2 + hk % 2
                        fns = [lambda e, kc=kc, hk=hk, bk=bk, tw=tw: e.matmul(
                            ps[bk][:, 0:tw], lhsT=wk[:, kc, hk, :], rhs=hT[:, kc, 0:tw],
                            start=(kc == 0), stop=(kc == KC - 1)) for kc in range(KC)]
                        P.run("pe", fns, reads=[w_b, hT_b], writes=[ps_b[bk]])
                        dst = kT_all[:, hk, t * 512:t * 512 + tw]
                        if hk % 2 == 0:
                            P.run("act", lambda e, dst=dst, bk=bk, tw=tw: e.activation(out=dst, in_=ps[bk][:, 0:tw], func=AF.Copy),
                                  reads=[ps_b[bk]], writes=[k_b[t]])
                        else:
                            P.run("dve", lambda e, dst=dst, bk=bk, tw=tw: e.tensor_copy(out=dst, in_=ps[bk][:, 0:tw]),
                                  reads=[ps_b[bk]], writes=[k_b[t]])
                    for sub in range(tw // 128):
                        bk = 4 + sub % 2
                        fns = [lambda e, kc=kc, sub=sub, bk=bk: e.matmul(
                            ps[bk][:, 0:256], lhsT=hT[:, kc, sub * 128:(sub + 1) * 128], rhs=wv[:, kc, :],
                            start=(kc == 0), stop=(kc == KC - 1)) for kc in range(KC)]
                        P.run("pe", fns, reads=[w_b, hT_b], writes=[ps_b[bk]])
                        dst = V_all[:, t * 4 + sub, :]
                        if sub % 2 == 0:
                            P.run("act", lambda e, dst=dst, bk=bk: e.activation(out=dst, in_=ps[bk][:, 0:256], func=AF.Copy),
                                  reads=[ps_b[bk]], writes=[v_b[t]])
                        else:
                            P.run("dve", lambda e, dst=dst, bk=bk: e.tensor_copy(out=dst, in_=ps[bk][:, 0:256]),
                                  reads=[ps_b[bk]], writes=[v_b[t]])
                P.barrier()
            with ExitStack() as ph2:
                def psb2(name, shape, dt=F32):
                    return ph2.enter_context(nc.sbuf_tensor(name, list(shape), dt))
                biasT = psb2("a_bias", [128, 3, 2048], F32)
                bias_b = Buf("a_bias")
                for js in range(3):
                    P.dma("sp", lambda e, js=js: e.dma_start(out=biasT[:, js, :], in_=biasT_d[:, js, :]), writes=[bias_b])
                wo = [psb2("a_wo%d" % i, [128, KC, 128], BF16) for i in range(2)]
                wo_b = [Buf("a_wo%d" % i) for i in range(2)]
                attnT = psb2("a_attnT", [128, KC, 512], BF16)
                attn_b = Buf("a_attn")
                PT = [[psb2("a_pt%d_%d" % (s_, j), [128, 512], BF16) for j in range(3)] for s_ in range(2)]
                pt_b = [[Buf("a_pt%d_%d" % (s_, j)) for j in range(3)] for s_ in range(2)]
                sc = [psb2("a_sc%d" % i, [128, 512], F32) for i in range(2)]
                sc_b = [Buf("a_sc%d" % i) for i in range(2)]
                rcp = [psb2("a_rcp%d" % i, [128, 256], F32) for i in range(2)]
                rcp_b = [Buf("a_rcp%d" % i) for i in range(2)]
                it = 0
                sci = 0
                wo_cnt = 0
                for t in range(4):
                    for bi in range(4):
                        blk = t * 4 + bi
                        jl = [1, 2] if blk == 0 else [0, 1, 2]
                        for hk in range(4):
                            st = it % 2
                            it += 1
                            for js in jl:
                                kb = blk - 1 + js
                                banks = (0, 1) if js != 1 else (2, 3)
                                fns = []
                                for g in range(4):
                                    h = 4 * hk + g
                                    c, par = h // 2, h % 2
                                    pi = g // 2
                                    fns.append(lambda e, pi=pi, c=c, par=par, kb=kb, banks=banks, hk=hk, blk=blk: e.matmul(
                                        ps[banks[par]][:, pi * 128:(pi + 1) * 128],
                                        lhsT=kT_all[par * 64:(par + 1) * 64, hk, kb * 128:(kb + 1) * 128],
                                        rhs=qT_all[par * 64:(par + 1) * 64, c, blk * 128:(blk + 1) * 128],
                                        start=True, stop=True))
                                P.run("pe", fns, reads=[k_b[kb // 4], q_b[t]], writes=[ps_b[banks[0]], ps_b[banks[1]]])
                                si = sci % 2
                                sci += 1
                                fns = []
                                for par in range(2):
                                    fns.append(lambda e, si=si, banks=banks, js=js, hk=hk, par=par: e.tensor_tensor(
                                        out=sc[si][:].rearrange("p (a b q) -> p a b q", a=2, b=2)[:, :, par, :],
                                        in0=ps[banks[par]][:, 0:256].rearrange("p (a q) -> p a q", a=2),
                                        in1=biasT[:, js, hk * 512:(hk + 1) * 512].rearrange("p (a b q) -> p a b q", a=2, b=2)[:, :, par, :],
                                        op=ALU.add))
                                P.run("dve", fns, reads=[ps_b[banks[0]], ps_b[banks[1]], bias_b], writes=[sc_b[si]])
                                P.run("act", lambda e, si=si, st=st, js=js: e.activation(
                                    out=PT[st][js][:], in_=sc[si][:], func=AF.Exp),
                                    reads=[sc_b[si]], writes=[pt_b[st][js]])
                            bpv = 4
                            bsum = 5
                            fns = []
                            for pi in range(2):
                                for par in range(2):
                                    g = 2 * pi + par
                                    for idx, js in enumerate(jl):
                                        kb = blk - 1 + js
                                        a0, a1 = (idx == 0), (idx == len(jl) - 1)
                                        fns.append(lambda e, bpv=bpv, par=par, pi=pi, kb=kb, hk=hk, st=st, js=js, g=g, a0=a0, a1=a1: e.matmul(
                                            ps[bpv][par * 64:(par + 1) * 64, pi * 128:(pi + 1) * 128],
                                            lhsT=V_all[:, kb, hk * 64:(hk + 1) * 64],
                                            rhs=PT[st][js][:, g * 128:(g + 1) * 128], start=a0, stop=a1))
                                        fns.append(lambda e, bsum=bsum, par=par, pi=pi, st=st, js=js, g=g, a0=a0, a1=a1: e.matmul(
                                            ps[bsum][par * 64:(par + 1) * 64, pi * 128:(pi + 1) * 128],
                                            lhsT=ones_bf[:, 0:64],
                                            rhs=PT[st][js][:, g * 128:(g + 1) * 128], start=a0, stop=a1))
                            P.run("pe", fns, reads=[v_b[(blk - 1 + js) // 4] for js in jl] + [pt_b[st][js] for js in jl] + [const_b],
                                  writes=[ps_b[bpv], ps_b[bsum]])
                            fns = []
                            for pi in range(2):
                                c = 2 * hk + pi
                                fns.append(lambda e, st=st, pi=pi, c=c, bsum=bsum: e.tensor_scalar(
                                    out=rcp[st][:, pi * 128:(pi + 1) * 128], in0=ps[bsum][:, pi * 128:(pi + 1) * 128],
                                    scalar1=esink[:, c:c + 1], scalar2=None, op0=ALU.add))
                            fns.append(lambda e, st=st: e.reciprocal(out=rcp[st][:], in_=rcp[st][:]))
                            fns.append(lambda e, st=st, hk=hk, bi=bi, bpv=bpv: e.tensor_tensor(
                                out=attnT[:, 2 * hk:2 * hk + 2, bi * 128:(bi + 1) * 128],
                                in0=ps[bpv][:, 0:256].rearrange("p (a q) -> p a q", a=2),
                                in1=rcp[st][:].rearrange("p (a q) -> p a q", a=2), op=ALU.mult))
                            P.run("dve", fns, reads=[ps_b[bpv], ps_b[bsum], esink_b], writes=[rcp_b[st], attn_b])
                    for oc in range(KC):
                        slot = wo_cnt % 2
                        wo_cnt += 1
                        P.dma("pool", lambda e, slot=slot, oc=oc: e.dma_start(
                            out=wo[slot][:], in_=att_wo_d[:, oc * 128:(oc + 1) * 128].rearrange("(kc p) n -> p kc n", p=128)),
                            writes=[wo_b[slot]])
                        bk = 6 + oc % 2
                        fns = [lambda e, kc=kc, slot=slot, bk=bk: e.matmul(
                            ps[bk][:, :], lhsT=wo[slot][:, kc, :], rhs=attnT[:, kc, :],
                            start=(kc == 0), stop=(kc == KC - 1)) for kc in range(KC)]
                        P.run("pe", fns, reads=[wo_b[slot], attn_b], writes=[ps_b[bk]])
                        P.run("act", [lambda e, oc=oc, bk=bk: e.activation(out=W["tmp"][:, oc, :], in_=ps[bk][:, :], func=AF.Copy),
                                      lambda e, oc=oc, bk=bk: e.activation(out=W["sq"][:, oc, :], in_=ps[bk][:, :], func=AF.Square)],
                              reads=[ps_b[bk]], writes=[W["tmp_b"], W["sq_b"]])
                    postnorm_residual(t, 0, 0, W)
                P.barrier()


    ret_win_d = dram("ret_w_in", [D, 6144])
    ret_wo_d = dram("ret_w_o", [2048, D])
    cos_d = dram("cosT", [128, TOK])
    sin_d = dram("sinT", [128, TOK])
    dlog_d = dram("dlog", [128, 8])
    emat_d = dram("emat", [128, 2, 128])
    mmat_d = dram("mmat", [128, 2, 128])
    xe_d = dram("xe", [128, 2, 128])
    ze_d = dram("ze", [128, 2])
    pmask_d = dram("pmask", [128, 2])
    QK_s = dram("QK_s", [16, 128, 16, 128], BF16, kind="Scratch")
    VG_s = dram("VG_s", [16, 128, 4096], BF16, kind="Scratch")
    YA_s = dram("YA_s", [16, 128, 2048], F32, kind="Scratch")
    if mode != "L2":
        ST_in = dram("ST_in", [1024, 512], F32, kind={"full": "Internal", "L1": "ExternalOutput"}[mode])
    if mode != "L1":
        ST_out = dram("ST_out", [2048, 512], F32, kind={"full": "Internal", "L2": "ExternalInput"}[mode])

    if mode == "L2" and DEBUG_DUMP:
        dbg_y = dram("dbg_y", [16, 128, 2048], F32, kind="ExternalOutput")
        dbg_yg = dram("dbg_yg", [16, 128, 2048], BF16, kind="ExternalOutput")
        dbg_s = dram("dbg_s", [1024, 512], F32, kind="ExternalOutput")

    def phase_ret():
        l = 1
        qk_sb = [Buf("QK_s%d" % c) for c in range(16)]
        vg_sb = [Buf("VG_s%d" % c) for c in range(16)]
        ya_sb = [Buf("YA_s%d" % c) for c in range(16)]
        stin_b = Buf("ST_in")
        stout_b = Buf("ST_out")
        with ExitStack() as ph:
            def psb(name, shape, dt=F32):
                return ph.enter_context(nc.sbuf_tensor(name, list(shape), dt))
            ident_bf = psb("r_identbf", [128, 128], BF16)
            s32_b = [Buf("r_s32_%d" % i) for i in range(4)]
            sbf_b = [Buf("r_sbf_%d" % i) for i in range(4)]
            lg = psb("r_lg", [128, 8], F32)
            lg2 = psb("r_lg2", [128, 8], F32)
            DT = psb("r_DT", [128, 8, 128], F32)
            XI = psb("r_XI", [128, 8, 128], F32)
            zeta = psb("r_zeta", [128, 8], F32)
            cdec = psb("r_cdec", [128, 8], F32)
            pmask = psb("r_pmask", [128, 2], F32)
            tab_b = Buf("r_tab")
            P.run("dve", lambda e: e.tensor_copy(out=ident_bf[:], in_=ident_f[:]), reads=[const_b], writes=[tab_b])
            with ExitStack() as ph0:
                def psb0(name, shape, dt=F32):
                    return ph0.enter_context(nc.sbuf_tensor(name, list(shape), dt))
                emat = psb0("r_emat", [128, 2, 128]); mmat = psb0("r_mmat", [128, 2, 128])
                xe = psb0("r_xe", [128, 2, 128]); ze = psb0("r_ze", [128, 2])
                ld_b = Buf("r_ld")
                for dst, src in ((lg, dlog_d), (emat, emat_d), (mmat, mmat_d), (xe, xe_d), (ze, ze_d), (pmask, pmask_d)):
                    P.dma("sp", lambda e, dst=dst, src=src: e.dma_start(out=dst[:], in_=src), writes=[ld_b])
                lgs_b = Buf("r_lgs")
                lga = psb0("r_lga", [128, 8]); lgb = psb0("r_lgb", [128, 8])
                P.run("act", lambda e: e.activation(out=lga[:], in_=lg[:], func=AF.Exp, scale=-1.0), reads=[ld_b], writes=[lgs_b])
                P.run("act", lambda e: e.activation(out=lgb[:], in_=lga[:], func=AF.Ln, bias=1.0), reads=[lgs_b], writes=[lgs_b])
                P.run("act", lambda e: e.activation(out=lg[:], in_=lgb[:], func=AF.Identity, scale=-1.0), reads=[lgs_b], writes=[tab_b, lgs_b])
                lg2_b = Buf("r_lg2")
                P.run("dve", lambda e: e.tensor_copy(out=lg2[:], in_=lg[:]), reads=[tab_b], writes=[lg2_b])
                fns = []
                for sl in range(2):
                    for h in range(4):
                        i = sl * 4 + h
                        fns.append(lambda e, sl=sl, i=i: e.activation(out=DT[:, i, :], in_=emat[:, sl, :], func=AF.Exp, scale=lg2[:, i:i + 1]))
                        fns.append(lambda e, sl=sl, i=i: e.activation(out=XI[:, i, :], in_=xe[:, sl, :], func=AF.Exp, scale=lg2[:, i:i + 1]))
                        fns.append(lambda e, sl=sl, i=i: e.activation(out=zeta[:, i:i + 1], in_=ze[:, sl:sl + 1], func=AF.Exp, scale=lg2[:, i:i + 1]))
                fns.append(lambda e: e.activation(out=cdec[:], in_=lg2[:], func=AF.Exp, scale=128.0))
                P.run("act", fns, reads=[ld_b, lg2_b], writes=[tab_b])
                fns = []
                for sl in range(2):
                    for h in range(4):
                        i = sl * 4 + h
                        fns.append(lambda e, sl=sl, i=i: e.tensor_tensor(out=DT[:, i, :], in0=DT[:, i, :], in1=mmat[:, sl, :], op=ALU.mult))
                fns.append(lambda e: e.tensor_scalar(out=zeta[:], in0=zeta[:], scalar1=0.0625, scalar2=None, op0=ALU.mult))
                P.run("dve", fns, reads=[ld_b], writes=[tab_b])
                P.barrier()
            with ExitStack() as ph1:
              if mode != "L2" and RET_STOP != "tables":
                  def psb1(name, shape, dt=F32):
                      return ph1.enter_context(nc.sbuf_tensor(name, list(shape), dt))
                  hT = psb1("r_hT", [128, KC, TOK], BF16)
                  hT_b = [Buf("r_hT%d" % i) for i in range(4)]
                  cosT = psb1("r_cos", [128, TOK]); sinT = psb1("r_sin", [128, TOK])
                  cs_b = Buf("r_cs")
                  P.dma("sp", lambda e: e.dma_start(out=cosT[:], in_=cos_d), writes=[cs_b])
                  P.dma("sp", lambda e: e.dma_start(out=sinT[:], in_=sin_d), writes=[cs_b])
                  W = dict(sq=psb1("r_sq", [128, KC, 512], BF16), sq_b=Buf("r_sq"),
                           tmp=psb1("r_tmp", [128, KC, 512], F32), tmp_b=Buf("r_tmp"),
                           rstd=psb1("r_rstd", [128, 512], F32), rstd_b=Buf("r_rstd"), nbank=7)
                  wr = [psb1("r_w%d" % i, [128, KC, 512], BF16) for i in range(2)]
                  wr_b = [Buf("r_w%d" % i) for i in range(2)]
                  rt = [psb1("r_rt%d" % i, [128, 512], F32) for i in range(4)]
                  rt_b = [Buf("r_rt%d" % i) for i in range(4)]
                  qkst = [psb1("r_qkst%d" % i, [128, 4, 512], BF16) for i in range(2)]
                  qkst_b = [Buf("r_qkst%d" % i) for i in range(2)]
                  vst = [psb1("r_vst%d" % i, [128, 512], BF16) for i in range(3)]
                  vst_b = [Buf("r_vst%d" % i) for i in range(3)]
                  for t in range(4):
                      prenorm(t, l, 0, hT[:, :, t * 512:(t + 1) * 512], hT_b[t], W)
                  vcnt = 0
                  qcnt = 0
                  for g in range(12):
                      slot = g % 2
                      P.dma("pool", lambda e, slot=slot, g=g: e.dma_start(
                          out=wr[slot][:], in_=ret_win_d[:, g * 512:(g + 1) * 512].rearrange("(kc p) n -> p kc n", p=128)),
                          writes=[wr_b[slot]])
                      for t in range(4):
                          if g < 4:
                              qs = qcnt % 2
                              qcnt += 1
                              for hh in range(2):
                                  for half in range(2):
                                      bk = half
                                      col = hh * 256 + half * 128
                                      fns = [lambda e, kc=kc, slot=slot, col=col, bk=bk, t=t: e.matmul(
                                          ps[bk][:, :], lhsT=wr[slot][:, kc, col:col + 128], rhs=hT[:, kc, t * 512:(t + 1) * 512],
                                          start=(kc == 0), stop=(kc == KC - 1)) for kc in range(KC)]
                                      P.run("pe", fns, reads=[wr_b[slot], hT_b[t]], writes=[ps_b[bk]])
                                  cs = cosT[:, t * 512:(t + 1) * 512]
                                  sn = sinT[:, t * 512:(t + 1) * 512]
                                  P.run("dve", [lambda e, cs=cs: e.tensor_tensor(out=rt[0][:], in0=ps[0][:, :], in1=cs, op=ALU.mult),
                                                lambda e, sn=sn: e.tensor_tensor(out=rt[1][:], in0=ps[1][:, :], in1=sn, op=ALU.mult),
                                                lambda e, sn=sn: e.tensor_tensor(out=rt[2][:], in0=ps[0][:, :], in1=sn, op=ALU.mult),
                                                lambda e, cs=cs: e.tensor_tensor(out=rt[3][:], in0=ps[1][:, :], in1=cs, op=ALU.mult)],
                                        reads=[ps_b[0], ps_b[1], cs_b], writes=rt_b)
                                  P.run("pool", [lambda e, qs=qs, hh=hh: e.tensor_tensor(out=qkst[qs][:, 2 * hh, :], in0=rt[0][:], in1=rt[1][:], op=ALU.subtract),
                                                 lambda e, qs=qs, hh=hh: e.tensor_tensor(out=qkst[qs][:, 2 * hh + 1, :], in0=rt[2][:], in1=rt[3][:], op=ALU.add)],
                                        reads=rt_b, writes=[qkst_b[qs]])
                              s0 = g * 4
                              for cc in range(4):
                                  c = t * 4 + cc
                                  P.dma("sp", lambda e, qs=qs, c=c, cc=cc, s0=s0: e.dma_start(
                                      out=QK_s[c, :, s0:s0 + 4, :], in_=qkst[qs][:, :, cc * 128:(cc + 1) * 128]),
                                      reads=[qkst_b[qs]], writes=[qk_sb[c]])
                          else:
                              for sub in range(4):
                                  c = t * 4 + sub
                                  bk = 2 + sub % 2
                                  vs = vcnt % 3
                                  vcnt += 1
                                  fns = [lambda e, kc=kc, slot=slot, bk=bk, c=c: e.matmul(
                                      ps[bk][:, :], lhsT=hT[:, kc, c * 128:(c + 1) * 128], rhs=wr[slot][:, kc, :],
                                      start=(kc == 0), stop=(kc == KC - 1)) for kc in range(KC)]
                                  P.run("pe", fns, reads=[wr_b[slot], hT_b[t]], writes=[ps_b[bk]])
                                  fn_ = AF.Copy if g < 8 else AF.Silu
                                  P.run("act", lambda e, vs=vs, bk=bk, fn_=fn_: e.activation(out=vst[vs][:], in_=ps[bk][:, :], func=fn_),
                                        reads=[ps_b[bk]], writes=[vst_b[vs]])
                                  col0 = (g - 4) * 512
                                  P.dma("sp", lambda e, vs=vs, c=c, col0=col0: e.dma_start(
                                      out=VG_s[c, :, col0:col0 + 512], in_=vst[vs][:]),
                                      reads=[vst_b[vs]], writes=[vg_sb[c]])
                  P.barrier()

            S32 = psb("r_S32", [128, 8, 512], F32)
            Sbf = psb("r_Sbf", [128, 8, 512], BF16)

            def scan(sl, W=None, extra=None):
                order = list(range(16)) if sl == 0 else list(range(15, -1, -1))
                with ExitStack() as ph2:
                    def psb2(name, shape, dt=F32):
                        return ph2.enter_context(nc.sbuf_tensor("sl%d_%s" % (sl, name), list(shape), dt))
                    qkc = [psb2("s_qk%d" % i, [128, 16, 128], BF16) for i in range(2)]
                    qkc_b = [Buf("s_qk%d" % i) for i in range(2)]
                    ncol = 2048 if sl == 0 else 4096
                    vgc = [psb2("s_vg%d" % i, [128, ncol], BF16) for i in range(2)]
                    vgc_b = [Buf("s_vg%d" % i) for i in range(2)]
                    yac = [psb2("s_ya%d" % i, [128, 2048], F32) for i in range(2)]
                    yac_b = [Buf("s_ya%d" % i) for i in range(2)]
                    sT = [psb2("s_sT%d" % i, [128, 128], BF16) for i in range(2)]
                    sT_b = [Buf("s_sT%d" % i) for i in range(2)]
                    qx = [psb2("s_qx%d" % i, [128, 2, 128], BF16) for i in range(2)]
                    qx_b = [Buf("s_qx%d" % i) for i in range(2)]
                    kz = [psb2("s_kz%d" % i, [128, 256], BF16) for i in range(2)]
                    kz_b = [Buf("s_kz%d" % i) for i in range(2)]
                    if sl == 1:
                        yy = [psb2("s_y%d" % i, [128, 512], F32) for i in range(2)]
                        yy_b = [Buf("s_y%d" % i) for i in range(2)]
                        yg = [psb2("s_yg%d" % i, [128, 512], BF16) for i in range(2)]
                        yg_b = [Buf("s_yg%d" % i) for i in range(2)]
                        ssq = psb2("s_ssq", [128, 8], F32)
                        ssq2 = psb2("s_ssq2", [128, 8], F32)
                        junk = psb2("s_junk", [128, 512], F32)
                        junk_b = Buf("s_junk"); ssq_b = Buf("s_ssq"); ssq2_b = Buf("s_ssq2"); ssq3_b = Buf("s_ssq3")
                        ssq3 = psb2("s_ssq3", [128, 8], F32)
                        ygT = psb2("s_ygT", [128, 16, 512], BF16)
                        ygT_b = Buf("s_ygT")
                        wo = [psb2("s_wo%d" % i, [128, 16, 128], BF16) for i in range(2)]
                        wo_b = [Buf("s_wo%d" % i) for i in range(2)]
                        wo_cnt = 0
                    hi = 0
                    for ci, c in enumerate(order):
                        cs = ci % 2
                        P.dma("sp", lambda e, cs=cs, c=c: e.dma_start(out=qkc[cs][:], in_=QK_s[c]), reads=[qk_sb[c]], writes=[qkc_b[cs]])
                        P.dma("sp", lambda e, cs=cs, c=c, ncol=ncol: e.dma_start(out=vgc[cs][:], in_=VG_s[c, :, 0:ncol]), reads=[vg_sb[c]], writes=[vgc_b[cs]])
                        if sl == 1:
                            P.dma("sp", lambda e, cs=cs, c=c: e.dma_start(out=yac[cs][:], in_=YA_s[c]), reads=[ya_sb[c]], writes=[yac_b[cs]])
                        for h in range(4):
                            i = sl * 4 + h
                            hs = hi % 2
                            hi += 1
                            bs = hs
                            fns = [lambda e, dkc=dkc, cs=cs, h=h, bs=bs: e.matmul(
                                ps[bs][:, 0:128], lhsT=qkc[cs][:, 8 + 2 * h + dkc, :], rhs=qkc[cs][:, 2 * h + dkc, :],
                                start=(dkc == 0), stop=(dkc == 1)) for dkc in range(2)]
                            P.run("pe", fns, reads=[qkc_b[cs]], writes=[ps_b[bs]])
                            P.run("dve", lambda e, hs=hs, bs=bs, i=i: e.tensor_tensor(out=sT[hs][:], in0=ps[bs][:, 0:128], in1=DT[:, i, :], op=ALU.mult),
                                  reads=[ps_b[bs], tab_b], writes=[sT_b[hs]])
                            P.run("pool", [lambda e, hs=hs, cs=cs, h=h, i=i, dkc=dkc: e.tensor_tensor(
                                out=qx[hs][:, dkc, :], in0=qkc[cs][:, 2 * h + dkc, :], in1=XI[:, i, :], op=ALU.mult) for dkc in range(2)],
                                reads=[qkc_b[cs], tab_b], writes=[qx_b[hs]])
                            bo = 2 + hs
                            fns = [lambda e, hs=hs, cs=cs, h=h, bo=bo: e.matmul(
                                ps[bo][:, :], lhsT=sT[hs][:], rhs=vgc[cs][:, h * 512:(h + 1) * 512], start=True, stop=False)]
                            for dkc in range(2):
                                fns.append(lambda e, hs=hs, h=h, bo=bo, dkc=dkc: e.matmul(
                                    ps[bo][:, :], lhsT=qx[hs][:, dkc, :], rhs=Sbf[:, 2 * h + dkc, :], start=False, stop=(dkc == 1)))
                            P.run("pe", fns, reads=[sT_b[hs], vgc_b[cs], qx_b[hs], sbf_b[h]], writes=[ps_b[bo]])
                            pkt = ps[4][:, 0:128].bitcast(BF16)
                            fns = [lambda e, cs=cs, h=h, dkc=dkc, pkt=pkt: e.transpose(
                                pkt[:, dkc * 128:(dkc + 1) * 128], qkc[cs][:, 8 + 2 * h + dkc, :], ident_bf[:]) for dkc in range(2)]
                            P.run("pe", fns, reads=[qkc_b[cs], tab_b], writes=[ps_b[4]])
                            P.run("act", lambda e, hs=hs, i=i, pkt=pkt: e.activation(out=kz[hs][:], in_=pkt, func=AF.Identity, scale=zeta[:, i:i + 1]),
                                  reads=[ps_b[4], tab_b], writes=[kz_b[hs]])
                            for dkc in range(2):
                                bu = 5 + dkc
                                P.run("pe", lambda e, hs=hs, cs=cs, h=h, dkc=dkc, bu=bu: e.matmul(
                                    ps[bu][:, :], lhsT=kz[hs][:, dkc * 128:(dkc + 1) * 128], rhs=vgc[cs][:, h * 512:(h + 1) * 512],
                                    start=True, stop=True), reads=[kz_b[hs], vgc_b[cs]], writes=[ps_b[bu]])
                            if sl == 0:
                                P.run("act", lambda e, cs=cs, h=h, bo=bo: e.activation(out=yac[cs][:, h * 512:(h + 1) * 512], in_=ps[bo][:, :], func=AF.Copy),
                                      reads=[ps_b[bo]], writes=[yac_b[cs]])
                            else:
                                P.run("dve", lambda e, hs=hs, cs=cs, h=h, bo=bo: e.tensor_tensor(
                                    out=yy[hs][:], in0=ps[bo][:, :], in1=yac[cs][:, h * 512:(h + 1) * 512], op=ALU.add),
                                    reads=[ps_b[bo], yac_b[cs]], writes=[yy_b[hs]])
                                if DEBUG_DUMP:
                                    P.dma("sp", lambda e, hs=hs, c=c, h=h: e.dma_start(out=dbg_y[c, :, h * 512:(h + 1) * 512], in_=yy[hs][:]), reads=[yy_b[hs]])
                                P.run("dve", lambda e, hs=hs, h=h: e.scalar_tensor_tensor(
                                    out=junk[:], in0=yy[hs][:], scalar=1.0, in1=yy[hs][:], op0=ALU.mult, op1=ALU.mult,
                                    accum_out=ssq[:, h:h + 1]), reads=[yy_b[hs]], writes=[junk_b, ssq_b])
                                P.run("act", lambda e, h=h: e.activation(out=ssq2[:, h:h + 1], in_=ssq[:, h:h + 1], func=AF.Sqrt,
                                                                         scale=1.0 / 512, bias=eps_t[:]),
                                      reads=[ssq_b], writes=[ssq2_b])
                                P.run("dve", lambda e, h=h: e.reciprocal(out=ssq3[:, h:h + 1], in_=ssq2[:, h:h + 1]),
                                      reads=[ssq2_b], writes=[ssq3_b])
                                P.run("act", lambda e, hs=hs, h=h: e.activation(out=junk[:], in_=yy[hs][:], func=AF.Identity,
                                                                              scale=ssq3[:, h:h + 1]),
                                      reads=[yy_b[hs], ssq3_b], writes=[junk_b])
                                P.run("dve", lambda e, hs=hs, cs=cs, h=h: e.tensor_tensor(
                                    out=yg[hs][:], in0=junk[:], in1=vgc[cs][:, 2048 + h * 512:2048 + (h + 1) * 512], op=ALU.mult),
                                    reads=[junk_b, vgc_b[cs]], writes=[yg_b[hs]])
                                if DEBUG_DUMP:
                                    P.dma("sp", lambda e, hs=hs, c=c, h=h: e.dma_start(out=dbg_yg[c, :, h * 512:(h + 1) * 512], in_=yg[hs][:]), reads=[yg_b[hs]])
                                pyt = ps[7][:, 0:256].bitcast(BF16)
                                fns = [lambda e, hs=hs, dvc=dvc, pyt=pyt: e.transpose(
                                    pyt[:, dvc * 128:(dvc + 1) * 128], yg[hs][:, dvc * 128:(dvc + 1) * 128], ident_bf[:]) for dvc in range(4)]
                                P.run("pe", fns, reads=[yg_b[hs], tab_b], writes=[ps_b[7]])
                                cc = c % 4
                                P.run("act", lambda e, h=h, cc=cc, pyt=pyt: e.activation(
                                    out=ygT[:, 4 * h:4 * h + 4, cc * 128:(cc + 1) * 128],
                                    in_=pyt.rearrange("p (a q) -> p a q", a=4), func=AF.Copy),
                                    reads=[ps_b[7]], writes=[ygT_b])
                            for dkc in range(2):
                                bu = 5 + dkc
                                P.run("dve", lambda e, h=h, dkc=dkc, bu=bu, i=i: e.scalar_tensor_tensor(
                                    out=S32[:, 2 * h + dkc, :], in0=S32[:, 2 * h + dkc, :], scalar=cdec[:, i:i + 1], in1=ps[bu][:, :],
                                    op0=ALU.mult, op1=ALU.add), reads=[ps_b[bu], tab_b], writes=[s32_b[h]])
                            P.run("pool", lambda e, h=h: e.tensor_copy(out=Sbf[:, 2 * h:2 * h + 2, :], in_=S32[:, 2 * h:2 * h + 2, :]),
                                  reads=[s32_b[h]], writes=[sbf_b[h]])
                        if sl == 0:
                            P.dma("sp", lambda e, cs=cs, c=c: e.dma_start(out=YA_s[c], in_=yac[cs][:]), reads=[yac_b[cs]], writes=[ya_sb[c]])
                        elif c % 4 == 0:
                            t = c // 4
                            for oc in range(KC):
                                slot = wo_cnt % 2
                                wo_cnt += 1
                                P.dma("pool", lambda e, slot=slot, oc=oc: e.dma_start(
                                    out=wo[slot][:], in_=ret_wo_d[:, oc * 128:(oc + 1) * 128].rearrange("(j p) n -> p j n", p=128)),
                                    writes=[wo_b[slot]])
                                bk = oc % 2
                                fns = [lambda e, j=j, slot=slot, bk=bk: e.matmul(
                                    ps[bk][:, :], lhsT=wo[slot][:, j, :], rhs=ygT[:, j, :],
                                    start=(j == 0), stop=(j == 15)) for j in range(16)]
                                P.run("pe", fns, reads=[wo_b[slot], ygT_b], writes=[ps_b[bk]])
                                P.run("act", [lambda e, oc=oc, bk=bk: e.activation(out=W["tmp"][:, oc, :], in_=ps[bk][:, :], func=AF.Copy),
                                              lambda e, oc=oc, bk=bk: e.activation(out=W["sq"][:, oc, :], in_=ps[bk][:, :], func=AF.Square)],
                                      reads=[ps_b[bk]], writes=[W["tmp_b"], W["sq_b"]])
                            postnorm_residual(t, l, 0, W)
                    P.barrier()

            if mode != "L2" and RET_STOP not in ("tables", "r1"):
                P.run("dve", lambda e: e.memset(S32[:], 0.0), writes=s32_b)
                P.run("pool", lambda e: e.memset(Sbf[:], 0.0), writes=sbf_b)
                scan(0)
                P.dma("sp", lambda e: e.dma_start(out=ST_in.rearrange("(j p) n -> p j n", p=128), in_=S32[:]), reads=s32_b, writes=[stin_b])
            if mode == "full":
                P.dma("pool", lambda e: e.collective_compute("AllGather", ALU.bypass, replica_groups=[[0, 1], [2, 3], [4, 5], [6, 7]],
                                                              ins=[ST_in], outs=[ST_out]), reads=[stin_b], writes=[stout_b])
            if mode == "L1":
                P.barrier()
                return
            with ExitStack() as ph3:
                s2 = ph3.enter_context(nc.sbuf_tensor("r_s2", [128, 8, 512], F32))
                s2_b = Buf("r_s2")
                P.dma("sp", lambda e: e.dma_start(out=S32[:], in_=ST_out[0:1024, :].rearrange("(j p) n -> p j n", p=128)), reads=[stout_b], writes=s32_b)
                P.dma("sp", lambda e: e.dma_start(out=s2[:], in_=ST_out[1024:2048, :].rearrange("(j p) n -> p j n", p=128)), reads=[stout_b], writes=[s2_b])
                P.run("dve", [lambda e: e.tensor_scalar(out=S32[:], in0=S32[:], scalar1=pmask[:, 0:1], scalar2=None, op0=ALU.mult),
                              lambda e: e.scalar_tensor_tensor(out=S32[:], in0=s2[:], scalar=pmask[:, 1:2], in1=S32[:], op0=ALU.mult, op1=ALU.add)],
                      reads=[s2_b, tab_b], writes=s32_b)
                P.run("act", lambda e: e.activation(out=Sbf[:], in_=S32[:], func=AF.Copy), reads=s32_b, writes=sbf_b)
                if DEBUG_DUMP:
                    P.dma("sp", lambda e: e.dma_start(out=dbg_s.rearrange("(j p) n -> p j n", p=128), in_=S32[:]), reads=s32_b)
                P.barrier()
            with ExitStack() as ph4:
                W = dict(sq=ph4.enter_context(nc.sbuf_tensor("rb_sq", [128, KC, 512], BF16)), sq_b=Buf("rb_sq"),
                         tmp=ph4.enter_context(nc.sbuf_tensor("rb_tmp", [128, KC, 512], F32)), tmp_b=Buf("rb_tmp"),
                         rstd=ph4.enter_context(nc.sbuf_tensor("rb_rstd", [128, 512], F32)), rstd_b=Buf("rb_rstd"), nbank=6)
                scan(1, W)

    def phase_output():
        with ExitStack() as ph:
            xo = [ph.enter_context(nc.sbuf_tensor("xo%d" % i, [128, D], F32)) for i in range(3)]
            xo_b = [Buf("xo%d" % i) for i in range(3)]
            for tb in range(16):
                s = tb % 3
                for half in range(2):
                    bk = (tb * 2 + half) % 4
                    fns = []
                    for i in range(4):
                        c = half * 4 + i
                        fns.append(lambda e, bk=bk, i=i, c=c, tb=tb: e.transpose(
                            ps[bk][:, i * 128:(i + 1) * 128], xT[:, c, tb * 128:(tb + 1) * 128], ident_f[:]))
                    P.run("pe", fns, reads=[xT_b[tb // 4], const_b], writes=[ps_b[bk]])
                    dst = xo[s][:, half * 512:(half + 1) * 512]
                    if half == 0:
                        P.run("dve", lambda e, dst=dst, bk=bk: e.tensor_copy(out=dst, in_=ps[bk][:, :]),
                              reads=[ps_b[bk]], writes=[xo_b[s]])
                    else:
                        P.run("act", lambda e, dst=dst, bk=bk: e.activation(out=dst, in_=ps[bk][:, :], func=AF.Copy),
                              reads=[ps_b[bk]], writes=[xo_b[s]])
                P.dma("sp", lambda e, s=s, tb=tb: e.dma_start(out=out_d[tb * 128:(tb + 1) * 128, :], in_=xo[s][:]),
                      reads=[xo_b[s]])
            P.wait_all("sp")

    phase_setup()
    if "att" in stages:
        phase_att()
    if "ffn0" in stages:
        phase_ffn(0)
    if "ret" in stages:
        phase_ret()
    if "ffn1" in stages:
        phase_ffn(1)
    phase_output()
    P.flush()
    es.close()
    return nc, in_names


def _t5_bucket(rel):
    nb = 16
    ret = np.where(rel > 0, nb, 0)
    n = np.abs(rel)
    max_exact = nb // 2
    nf = np.maximum(n, 1).astype(np.float32)
    large = max_exact + (np.log(nf / np.float32(max_exact)) / np.float32(math.log(128 / max_exact))
                         * np.float32(nb - max_exact)).astype(np.int32)
    large = np.minimum(large, nb - 1)
    return ret + np.where(n < max_exact, n, large)


def _bias_tables(rel_bias):
    t = np.arange(128, dtype=np.int32)
    j = np.arange(384, dtype=np.int32)
    rel = j[None, :] - 128 - t[:, None]
    bias = rel_bias[_t5_bucket(rel)]
    bias = np.where((np.abs(rel) <= 128)[:, :, None], bias, np.float32(-1e30)).astype(np.float32)
    b4 = bias.reshape(128, 3, 128, 16)
    fwd = np.ascontiguousarray(b4.transpose(2, 1, 3, 0).reshape(128, 3, 2048))
    rev = np.ascontiguousarray(b4[:, ::-1].transpose(2, 1, 3, 0).reshape(128, 3, 2048))
    return fwd, rev


def _host_inputs(inputs):
    x = np.ascontiguousarray(inputs["x"], dtype=np.float32)
    c = np.asarray(inputs["c"], dtype=np.float32)
    ident = np.eye(128, dtype=np.float32)
    ada_w = np.ascontiguousarray(inputs["ada_w"], dtype=np.float32)
    ada_bT = np.ascontiguousarray(np.asarray(inputs["ada_b"], np.float32).reshape(2, 48, 128).transpose(0, 2, 1))
    norms = np.stack([np.asarray(inputs[k], np.float32) for k in
                      ("mix_norm_pre", "mix_norm_post", "ffn_norm_pre", "ffn_norm_post")], axis=1)
    normsT = np.ascontiguousarray(norms.reshape(2, 4, KC, 128).transpose(0, 1, 3, 2))
    ffn_w_in = np.ascontiguousarray(inputs["ffn_w_in"], dtype=np.float32)
    ffn_w_out = np.ascontiguousarray(inputs["ffn_w_out"], dtype=np.float32)
    att_w_qkv = np.ascontiguousarray(inputs["att_w_qkv"][0], dtype=np.float32)
    att_w_o = np.ascontiguousarray(inputs["att_w_o"][0], dtype=np.float32)
    sink = np.asarray(inputs["att_sink"][0], np.float32)
    sinkT = np.ascontiguousarray(np.stack([sink[2 * np.arange(KC) + (p // 64)] for p in range(128)], axis=0))
    biasT_fwd, biasT_rev = _bias_tables(np.asarray(inputs["rel_bias"], np.float32))
    ret_w_in = np.ascontiguousarray(inputs["ret_w_in"][0], dtype=np.float32)
    ret_w_o = np.ascontiguousarray(inputs["ret_w_o"][0], dtype=np.float32)
    dfwd = np.asarray(inputs["ret_decay_fwd"][0], np.float32)
    dbwd = np.asarray(inputs["ret_decay_bwd"][0], np.float32)
    inv = (np.float32(10000.0) ** (-np.arange(0, 256, 2, dtype=np.float32) / np.float32(256))).astype(np.float32)
    pos = np.arange(4096, dtype=np.float32)
    ang = (pos[:, None] * inv[None]).astype(np.float32)
    cos_full = np.cos(ang).astype(np.float32).T
    sin_full = np.sin(ang).astype(np.float32).T
    li = np.arange(128, dtype=np.float32)
    e_f = np.maximum(li[None, :] - li[:, None], 0.0).astype(np.float32)
    m_f = (li[None, :] >= li[:, None]).astype(np.float32) * np.float32(0.0625)
    e_b = np.maximum(li[:, None] - li[None, :], 0.0).astype(np.float32)
    m_b = (li[:, None] > li[None, :]).astype(np.float32) * np.float32(0.0625)
    xe_f = np.broadcast_to((li + 1.0)[None, :], (128, 128)).astype(np.float32)
    xe_b = np.broadcast_to((128.0 - li)[None, :], (128, 128)).astype(np.float32)
    ze_f = (127.0 - li).astype(np.float32)
    ze_b = li.astype(np.float32)
    maps = []
    for r in range(NCORES):
        b, half = r // 2, r % 2
        xb = x[b].reshape(32, 128, D)
        order = list(range(0, 17)) if half == 0 else list(range(31, 14, -1))
        m = {
            "x": np.ascontiguousarray(xb[order].reshape(17 * 128, D)),
            "att_w_qkv": att_w_qkv, "att_w_o": att_w_o, "sinkT": sinkT,
            "biasT": biasT_fwd if half == 0 else biasT_rev,
            "ret_w_in": ret_w_in, "ret_w_o": ret_w_o,
            "cosT": np.ascontiguousarray(cos_full.reshape(128, 32, 128)[:, order[:16], :].reshape(128, TOK)),
            "sinT": np.ascontiguousarray(sin_full.reshape(128, 32, 128)[:, order[:16], :].reshape(128, TOK)),
            "dlog": np.ascontiguousarray(np.broadcast_to(
                (np.concatenate([dfwd, dbwd]) if half == 0 else np.concatenate([dbwd, dfwd]))[None, :], (128, 8))),
            "emat": np.ascontiguousarray(np.stack([e_f, e_b] if half == 0 else [e_b, e_f], axis=1)),
            "mmat": np.ascontiguousarray(np.stack([m_f, m_b] if half == 0 else [m_b, m_f], axis=1)),
            "xe": np.ascontiguousarray(np.stack([xe_f, xe_b] if half == 0 else [xe_b, xe_f], axis=1)),
            "ze": np.ascontiguousarray(np.stack([ze_f, ze_b] if half == 0 else [ze_b, ze_f], axis=1)),
            "pmask": np.ascontiguousarray(np.broadcast_to(
                np.array([0.0, 1.0] if half == 0 else [1.0, 0.0], np.float32)[None, :], (128, 2))),
            "ident": ident,
            "cT": np.ascontiguousarray(c[b].reshape(KC, 128).T),
            "ada_w": ada_w,
            "ada_bT": ada_bT,
            "normsT": normsT,
            "ffn_w_in": ffn_w_in,
            "ffn_w_out": ffn_w_out,
        }
        maps.append(m)
    return maps


def _scatter_out(out, r, o):
    b, half = r // 2, r % 2
    o = np.asarray(o).reshape(16, 128, D)
    if half == 0:
        out[b, 0:TOK] = o.reshape(TOK, D)
    else:
        out[b, TOK:2 * TOK] = o[::-1].reshape(TOK, D)


_STAGES = ("att", "ffn0", "ret", "ffn1")


def kernel(**inputs):
    maps = _host_inputs(inputs)
    nc1, names1 = build_program(("att", "ffn0", "ret"), mode="L1")
    res1 = run_bass_kernel_spmd(nc1, [{k: m[k] for k in names1} for m in maps], core_ids=list(range(NCORES)))
    r1 = res1.results
    nc2, names2 = build_program(("ret", "ffn1"), mode="L2")
    maps2 = []
    zero_halo = np.zeros((128, D), np.float32)
    for r in range(NCORES):
        m = dict(maps[r])
        m["x"] = np.concatenate([np.asarray(r1[r]["out"], np.float32), zero_halo], axis=0)
        for k in ("QK_s", "VG_s", "YA_s"):
            m[k] = np.asarray(r1[r][k])
        pr = (r // 2) * 2
        m["ST_out"] = np.concatenate([np.asarray(r1[pr]["ST_in"]), np.asarray(r1[pr + 1]["ST_in"])], axis=0)
        maps2.append({k: m[k] for k in names2})
    res = run_bass_kernel_spmd(nc2, maps2, core_ids=list(range(NCORES)))
    out = np.empty((4, 4096, D), dtype=np.float32)
    for r in range(NCORES):
        _scatter_out(out, r, res.results[r]["out"])
    return out
```
